# Optimizing a Trainium2 kernel written in Bass

```python
import jax, jax.numpy as jnp
from jax import lax
import numpy as np

D_MODEL = 1024
BATCH = 2
SEQ = 8192
DEPTH = 4

N_MIXERS = 2
N_A = (DEPTH + 1) // 2
N_B = DEPTH // 2
EPS = 1e-6

CHUNK = 128
A_WIDTH = 2 * D_MODEL
A_GROUPS = 8
A_GDIM = A_WIDTH // A_GROUPS

B_WIDTH = 3 * D_MODEL // 2
B_HEADS = 6
B_HDIM = B_WIDTH // B_HEADS
B_CONV = 4
LRU_C = 8.0

D_FF = 3 * D_MODEL
FFN_CONV = 3

kernel_name = "hybrid_gmlp_rglru_convffn"


def rms_norm(x, g):
    xf = x.astype(jnp.float32)
    y = xf * lax.rsqrt(jnp.mean(xf * xf, axis=-1, keepdims=True) + EPS)
    return (y * g.astype(jnp.float32)).astype(x.dtype)


def causal_dwconv(x, w, b):
    k, c = w.shape
    y = lax.conv_general_dilated(
        x, w.astype(x.dtype)[:, None, :], window_strides=(1,), padding=[(k - 1, 0)],
        dimension_numbers=("NWC", "WIO", "NWC"), feature_group_count=c)
    return y + b.astype(x.dtype)


def mixer_a(h, w_in, v_gain, w_s, b_s, w_out):
    bsz, s, _ = h.shape
    z = jax.nn.gelu(h @ w_in)
    u, v = jnp.split(z, 2, axis=-1)
    v = rms_norm(v, v_gain)
    v = v.reshape(bsz, s // CHUNK, CHUNK, A_GROUPS, A_GDIM)
    ws = jnp.tril(w_s).astype(v.dtype)
    sv = jnp.einsum("gts,bcsgd->bctgd", ws, v) + b_s.T.astype(v.dtype)[:, :, None]
    y = u * sv.reshape(bsz, s, A_WIDTH)
    return y @ w_out


def _lin_rec_op(c1, c2):
    a1, b1 = c1
    a2, b2 = c2
    return a1 * a2, a2 * b1 + b2


def mixer_b(h, w_in, conv_w, conv_b, w_a, b_a, w_x, b_x, lam, w_out):
    bsz, s, _ = h.shape
    g, xb = jnp.split(h @ w_in, 2, axis=-1)
    g = jax.nn.gelu(g)
    xb = causal_dwconv(xb, conv_w, conv_b)
    xh = xb.reshape(bsz, s, B_HEADS, B_HDIM)
    r = jax.nn.sigmoid(jnp.einsum("bshd,hde->bshe", xh, w_a) + b_a).reshape(bsz, s, B_WIDTH)
    i = jax.nn.sigmoid(jnp.einsum("bshd,hde->bshe", xh, w_x) + b_x).reshape(bsz, s, B_WIDTH)
    log_a = LRU_C * r.astype(jnp.float32) * jax.nn.log_sigmoid(lam.astype(jnp.float32))
    a = jnp.exp(log_a)
    mult = jnp.sqrt(-jnp.expm1(2.0 * log_a))
    bterm = mult * (i * xb).astype(jnp.float32)
    _, hs = lax.associative_scan(_lin_rec_op, (a, bterm), axis=1)
    y = hs.astype(h.dtype) * g
    return y @ w_out


def conv_ffn(h, w_up, conv_w, conv_b, w_down):
    up = causal_dwconv(h @ w_up, conv_w, conv_b)
    gate, val = jnp.split(up, 2, axis=-1)
    return (jax.nn.gelu(gate) * val) @ w_down


def setup_inputs(seed: int = 0) -> dict:
    key = jax.random.key(seed)
    ks = jax.random.split(key, 24)
    nrm = jax.random.normal
    f32 = jnp.float32

    x = nrm(ks[0], (BATCH, SEQ, D_MODEL), f32)
    norm_mix = 1.0 + 0.05 * nrm(ks[1], (DEPTH, D_MODEL), f32)
    norm_ffn = 1.0 + 0.05 * nrm(ks[2], (DEPTH, D_MODEL), f32)
    norm_final = 1.0 + 0.05 * nrm(ks[3], (D_MODEL,), f32)

    a_w_in = nrm(ks[4], (N_A, D_MODEL, 2 * A_WIDTH), f32) * D_MODEL ** -0.5
    a_v_gain = 1.0 + 0.05 * nrm(ks[5], (N_A, A_WIDTH), f32)
    a_w_s = nrm(ks[6], (N_A, A_GROUPS, CHUNK, CHUNK), f32) * (0.5 * CHUNK ** -0.5)
    a_b_s = 1.0 + 0.1 * nrm(ks[7], (N_A, A_GROUPS, CHUNK), f32)
    a_w_out = nrm(ks[8], (N_A, A_WIDTH, D_MODEL), f32) * A_WIDTH ** -0.5

    b_w_in = nrm(ks[9], (N_B, D_MODEL, 2 * B_WIDTH), f32) * D_MODEL ** -0.5
    b_conv_w = nrm(ks[10], (N_B, B_CONV, B_WIDTH), f32) * B_CONV ** -0.5
    b_conv_b = 0.01 * nrm(ks[11], (N_B, B_WIDTH), f32)
    b_w_a = nrm(ks[12], (N_B, B_HEADS, B_HDIM, B_HDIM), f32) * B_HDIM ** -0.5
    b_b_a = 0.01 * nrm(ks[13], (N_B, B_HEADS, B_HDIM), f32)
    b_w_x = nrm(ks[14], (N_B, B_HEADS, B_HDIM, B_HDIM), f32) * B_HDIM ** -0.5
    b_b_x = 0.01 * nrm(ks[15], (N_B, B_HEADS, B_HDIM), f32)
    a0 = jax.random.uniform(ks[16], (N_B, B_WIDTH), f32, 0.9, 0.999)
    s0 = a0 ** (1.0 / LRU_C)
    b_lambda = jnp.log(s0) - jnp.log1p(-s0)
    b_w_out = nrm(ks[17], (N_B, B_WIDTH, D_MODEL), f32) * B_WIDTH ** -0.5

    f_w_up = nrm(ks[18], (DEPTH, D_MODEL, 2 * D_FF), f32) * D_MODEL ** -0.5
    f_conv_w = nrm(ks[19], (DEPTH, FFN_CONV, 2 * D_FF), f32) * FFN_CONV ** -0.5
    f_conv_b = 0.01 * nrm(ks[20], (DEPTH, 2 * D_FF), f32)
    f_w_down = nrm(ks[21], (DEPTH, D_FF, D_MODEL), f32) * D_FF ** -0.5

    return {"x": x, "norm_mix": norm_mix, "norm_ffn": norm_ffn, "norm_final": norm_final,
            "a_w_in": a_w_in, "a_v_gain": a_v_gain, "a_w_s": a_w_s, "a_b_s": a_b_s, "a_w_out": a_w_out,
            "b_w_in": b_w_in, "b_conv_w": b_conv_w, "b_conv_b": b_conv_b, "b_w_a": b_w_a, "b_b_a": b_b_a,
            "b_w_x": b_w_x, "b_b_x": b_b_x, "b_lambda": b_lambda, "b_w_out": b_w_out,
            "f_w_up": f_w_up, "f_conv_w": f_conv_w, "f_conv_b": f_conv_b, "f_w_down": f_w_down}


def reference(x, norm_mix, norm_ffn, norm_final,
              a_w_in, a_v_gain, a_w_s, a_b_s, a_w_out,
              b_w_in, b_conv_w, b_conv_b, b_w_a, b_b_a, b_w_x, b_b_x, b_lambda, b_w_out,
              f_w_up, f_conv_w, f_conv_b, f_w_down):
    for i in range(DEPTH):
        h = rms_norm(x, norm_mix[i])
        j = i // N_MIXERS
        if i % N_MIXERS == 0:
            x = x + mixer_a(h, a_w_in[j], a_v_gain[j], a_w_s[j], a_b_s[j], a_w_out[j])
        else:
            x = x + mixer_b(h, b_w_in[j], b_conv_w[j], b_conv_b[j], b_w_a[j], b_b_a[j],
                            b_w_x[j], b_b_x[j], b_lambda[j], b_w_out[j])
        h = rms_norm(x, norm_ffn[i])
        x = x + conv_ffn(h, f_w_up[i], f_conv_w[i], f_conv_b[i], f_w_down[i])
    return rms_norm(x, norm_final)
```

```python
import os
import numpy as np
import concourse.bass as bass
import concourse.mybir as mybir
from concourse.bass_utils import run_bass_kernel_spmd

F32 = mybir.dt.float32
BF16 = mybir.dt.bfloat16
AF = mybir.ActivationFunctionType
ALU = mybir.AluOpType

NCORES = 8
D = 1024
KD = 8
T = 2048
HALO = 3
XC = T + HALO
DEPTH = 4
EPS = 1e-6
NSLOT = 4
SLOTW = 4096
AW = 24500
NDMASEM = 8
DBG = os.environ.get("BDBG", "")

TILES = [(0, 410), (410, 410), (820, 410), (1230, 410), (1640, 408)]
UNITS = [(0, 2, 410), (2, 2, 410), (4, 1, 408)]


class Op:
    __slots__ = ("eng", "fn", "deps", "kind", "sem", "val", "signal", "seq", "waits", "prevval", "epoch")


class Prog:
    ENGS = ("pe", "act", "dve", "pool", "sp")

    def __init__(self):
        self.ops = {e: [] for e in self.ENGS}
        self.lastw = {}
        self.readers = {}
        self.epoch = 0
        self.markers = {}

    def add(self, eng, fn, r=(), w=(), kind="c", extra=()):
        op = Op()
        op.eng, op.fn, op.kind = eng, fn, kind
        op.deps = set(extra)
        op.signal = False
        op.sem = None
        op.val = 0
        op.seq = 0
        op.prevval = 0
        op.epoch = self.epoch
        for k in r:
            lw = self.lastw.get(k)
            if lw is not None:
                op.deps.add(lw)
        for k in w:
            lw = self.lastw.get(k)
            if lw is not None:
                op.deps.add(lw)
            for rd in self.readers.get(k, ()):
                op.deps.add(rd)
        for k in r:
            self.readers.setdefault(k, []).append(op)
        for k in w:
            self.lastw[k] = op
            self.readers[k] = []
        op.deps.discard(op)
        self.ops[eng].append(op)
        return op

    def fence(self, prefix):
        deps = set()
        for k in list(self.lastw.keys()):
            if isinstance(k, tuple) and k[0] == prefix:
                deps.add(self.lastw.pop(k))
        for k in list(self.readers.keys()):
            if isinstance(k, tuple) and k[0] == prefix:
                deps.update(self.readers.pop(k))
        if deps:
            for e in self.ENGS:
                self.add(e, None, extra=deps)

    def finalize(self, eng_sems, dma_sems, cc_sem):
        for e in self.ENGS:
            for op in self.ops[e]:
                extra = set()
                for d in op.deps:
                    if d.kind == "dma" and d.eng == "pool" and d.epoch > 0 and op.eng != "pool":
                        extra.add(self.markers[d.epoch])
                op.deps |= extra
        for e in self.ENGS:
            for op in self.ops[e]:
                for d in op.deps:
                    if d.kind == "c" and not (d.eng == "pe" and op.eng == "pe"):
                        d.signal = True
        ncc = 0
        for e in self.ENGS:
            c = 0
            nd = 0
            for op in self.ops[e]:
                if op.kind == "reset":
                    nd = 0
                elif op.kind == "dma":
                    op.sem = dma_sems[e][nd % NDMASEM]
                    op.val = 16 * (nd // NDMASEM + 1)
                    op.prevval = op.val - 16
                    nd += 1
                elif op.kind == "cc":
                    ncc += 1
                    op.sem = cc_sem
                    op.val = ncc
                elif op.signal:
                    c += 1
                    op.seq = c
        for e in self.ENGS:
            known = {}
            for op in self.ops[e]:
                need = {}
                for d in op.deps:
                    if d.kind == "reset":
                        continue
                    if d.kind == "dma" and d.eng == "pool" and d.epoch < op.epoch:
                        continue
                    ep = 0
                    if d.kind == "c":
                        if d.eng == "pe" and op.eng == "pe":
                            continue
                        assert d.fn is not None
                        s, v = eng_sems[d.eng], d.seq
                    else:
                        s, v = d.sem, d.val
                        if d.kind == "dma" and d.eng == "pool":
                            ep = d.epoch
                    if v > need.get((id(s), ep), (None, 0))[1]:
                        need[(id(s), ep)] = (s, v)
                if op.kind == "dma" and op.prevval > 0:
                    s, v = op.sem, op.prevval
                    ep = op.epoch if op.eng == "pool" else 0
                    if v > need.get((id(s), ep), (None, 0))[1]:
                        need[(id(s), ep)] = (s, v)
                op.waits = []
                for sid, (s, v) in sorted(need.items(), key=lambda kv: 0 if any(kv[1][0] is es_ for es_ in eng_sems.values()) else 1):
                    if known.get(sid, 0) < v:
                        known[sid] = v
                        op.waits.append((s, v))

    def emit(self, eng, e, eng_sems):
        for op in self.ops[eng]:
            for (s, v) in op.waits:
                e.wait_ge(s, v)
            if op.fn is None:
                continue
            ins = op.fn(e)
            if op.kind == "reset":
                continue
            if op.kind == "dma":
                ins.then_inc(op.sem, 16)
            elif op.kind == "cc":
                ins.then_inc(op.sem)
            elif op.signal:
                ins.then_inc(eng_sems[eng], 1)


def _cols(v):
    v = np.asarray(v, np.float32).reshape(-1)
    return np.ascontiguousarray(v.reshape(-1, 128).T)


def pack_consts(inp):
    cols = {}
    parts = []
    off = [0]

    def put(name, arr):
        cols[name] = off[0]
        parts.append(arr.astype(np.float32))
        off[0] += arr.shape[1]

    for i in range(DEPTH):
        put(f"nm{i}", _cols(inp["norm_mix"][i]))
        put(f"nf{i}", _cols(inp["norm_ffn"][i]))
        for k in range(3):
            put(f"fcw{i}_{k}", _cols(inp["f_conv_w"][i, k]))
        put(f"fcb{i}", _cols(inp["f_conv_b"][i]))
    put("nfin", _cols(inp["norm_final"]))
    for j in range(2):
        put(f"avg{j}", _cols(inp["a_v_gain"][j]))
        for k in range(4):
            put(f"bcw{j}_{k}", _cols(inp["b_conv_w"][j, k]))
        put(f"bcb{j}", _cols(inp["b_conv_b"][j]))
        put(f"bba{j}", _cols(inp["b_b_a"][j]))
        put(f"bbx{j}", _cols(inp["b_b_x"][j]))
        put(f"blam{j}", _cols(inp["b_lambda"][j]))
    put("selp", np.zeros((128, 8), np.float32))
    put("selq", np.zeros((128, 8), np.float32))
    base = np.concatenate(parts, axis=1)
    per_core = []
    for c in range(NCORES):
        a = base.copy()
        p = c % 4
        if p > 0:
            a[:, cols["selp"] + c - 1] = 1.0
            a[:, cols["selq"] + c - p: cols["selq"] + c] = 1.0
        per_core.append(np.ascontiguousarray(a))
    return cols, per_core


def build_program(phases, cols, ncons, load_x=True):
    nc = bass.Bass("TRN2", target_bir_lowering=False)
    used_inputs = {}

    def dt_in(name, shape):
        if name not in used_inputs:
            used_inputs[name] = nc.dram_tensor(name, shape, F32, kind="ExternalInput").ap()
        return used_inputs[name]

    SHAPES = {"a_w_in": [2, 1024, 4096], "a_w_out": [2, 2048, 1024], "b_w_in": [2, 1024, 3072],
              "b_w_out": [2, 1536, 1024], "b_w_a": [2, 6, 256, 256], "b_w_x": [2, 6, 256, 256],
              "f_w_up": [4, 1024, 6144], "f_w_down": [4, 3072, 1024], "bsb": [2, 128, 1024],
              "wsT": [2, 128, 1024]}
    W = lambda name: dt_in(name, SHAPES[name])
    x_in = dt_in("xT", [D, T])
    consts_in = dt_in("consts", [128, ncons])
    y_out = nc.dram_tensor("yT", [D, T], F32, kind="ExternalOutput").ap()

    n_x = sum(1 for p in phases if p[0] in ("X", "B"))
    bounce_in = [nc.dram_tensor(f"bin{i}", [128, 24], F32) for i in range(n_x)]
    bounce_out = [nc.dram_tensor(f"bout{i}", [NCORES * 128, 24], F32) for i in range(n_x)]

    P = Prog()

    import contextlib
    with contextlib.ExitStack() as es:
        sb = lambda name, shape, dt: es.enter_context(nc.sbuf_tensor(name, shape, dt))
        consts = sb("consts_sb", [128, ncons], F32)
        xT = sb("xT_sb", [128, KD, XC], F32)
        ones = sb("ones_sb", [128, 128], BF16)
        ring = sb("ring_sb", [128, NSLOT, SLOTW], BF16)
        arena = sb("arena_sb", [128, AW], F32)
        misc = sb("misc_sb", [128, 1024], F32)
        rst_mark = sb("rst_mark", [128, 8], F32)
        ps = es.enter_context(nc.psum_tensor("ps", [128, 8, 512], F32))
        eng_sems = {e: es.enter_context(nc.semaphore(f"s_{e}")) for e in Prog.ENGS}
        dma_sems = {e: [es.enter_context(nc.semaphore(f"d_{e}{i}")) for i in range(NDMASEM)]
                    for e in ("sp",)}
        dma_sems["act"] = dma_sems["sp"]
        dma_sems["pool"] = dma_sems["sp"]
        for e in ("pe", "dve"):
            dma_sems[e] = dma_sems["sp"]
        dma_sems["pool"] = [es.enter_context(nc.semaphore(f"d_pool{i}")) for i in range(NDMASEM)]
        cc_sem = es.enter_context(nc.semaphore("cc"))

        def ccol(name, k=0, n=1):
            o = cols[name] + k
            return consts[:, o:o + n]

        class Arena:
            def __init__(self):
                self.off = 0

            def reset(self):
                self.off = 0

            def f32(self, n):
                v = arena[:, self.off:self.off + n]
                self.off += n
                assert self.off <= AW, ("arena overflow", self.off)
                return v

            def bf16(self, n):
                w = (n + 1) // 2
                v = arena[:, self.off:self.off + w].bitcast(BF16)
                self.off += w
                assert self.off <= AW, ("arena overflow", self.off)
                return v[:, 0:n]

        A = Arena()
        mo = [0]

        def mcol(n):
            v = misc[:, mo[0]:mo[0] + n]
            mo[0] += n
            assert mo[0] <= 1024
            return v

        ring_state = {"n": 0}

        def load_piece(dmas):
            i = ring_state["n"]
            ring_state["n"] += 1
            s = i % NSLOT
            slot = ring[:, s, :]
            key = ("ring", s)
            for (dst_fn, src) in dmas:
                dst = dst_fn(slot)
                P.add("pool", lambda e, dst=dst, src=src: e.dma_start(out=dst, in_=src),
                      w=[key], kind="dma")
            return slot, key

        def pool_reset():
            bank = dma_sems["pool"]
            lo, hi = bank[0].num, bank[-1].num + 1

            def do_reset(e):
                e.dma_reset(semaphore_range=range(lo, hi))
                return e.sem_clear(range(lo, hi))
            P.add("pool", do_reset, w=[("ring", s_) for s_ in range(NSLOT)], kind="reset")
            P.epoch += 1
            P.markers[P.epoch] = P.add("pool", lambda e: e.memset(rst_mark[:, :], 0.0), w=["rstmark"])

        P.add("sp", lambda e: e.dma_start(out=consts[:, :], in_=consts_in[:, :]), w=["consts"], kind="dma")
        P.add("pool", lambda e: e.memset(ones[:, :], 1.0), w=["ones"])
        XK = [("x", m, h) for m in range(KD) for h in range(2)]
        if load_x:
            xin_v = x_in.rearrange("(k p) t -> p k t", p=128)
            for k in range(KD):
                P.add("sp", lambda e, k=k: e.dma_start(out=xT[:, k, HALO:XC], in_=xin_v[:, k, :]),
                      w=[("x", k, 0), ("x", k, 1)], kind="dma")
        P.add("pool", lambda e: e.memset(xT[:, :, 0:HALO], 0.0), w=["xh"])

        def xkeys(c0, c1):
            ks = []
            if c0 < HALO:
                ks.append("xh")
            for h in range(2):
                lo, hi = HALO + 1024 * h, HALO + 1024 * (h + 1)
                if c0 < hi and c1 > lo:
                    ks += [("x", m, h) for m in range(KD)]
            return ks

        def emit_norm(gname, c0, ncols, dst, dkey, sq, sqkey, rt, rtkey, bank, tilew, out_f32=False):
            j = 0
            while j < ncols:
                w = min(tilew, ncols - j)
                a = c0 + j
                P.add("act", lambda e, a=a, w=w: e.activation(out=sq[:, :, 0:w], in_=xT[:, :, a:a + w], func=AF.Square),
                      r=xkeys(a, a + w), w=[sqkey])
                for k in range(KD):
                    P.add("pe", lambda e, k=k, w=w: e.matmul(ps[:, bank, 0:w], lhsT=ones[:, :], rhs=sq[:, k, 0:w],
                                                              start=(k == 0), stop=(k == KD - 1)),
                          r=["ones", sqkey], w=[("ps", bank)])
                P.add("act", lambda e, w=w: e.activation(out=rt[:, 0:w], in_=ps[:, bank, 0:w], func=AF.Sqrt,
                                                         scale=1.0 / D, bias=EPS),
                      r=[("ps", bank)], w=[rtkey])
                P.add("dve", lambda e, w=w: e.reciprocal(out=ps[:, bank, 0:w], in_=rt[:, 0:w]),
                      r=[rtkey], w=[("ps", bank)])
                for k in range(KD):
                    P.add("dve", lambda e, k=k, a=a, j=j, w=w: e.scalar_tensor_tensor(
                        out=dst[:, k, j:j + w], in0=xT[:, k, a:a + w], scalar=ccol(gname, k), in1=ps[:, bank, 0:w],
                        op0=ALU.mult, op1=ALU.mult),
                        r=xkeys(a, a + w) + [("ps", bank), "consts"], w=[dkey])
                j += w

        def evac_add(bank, m, t0, w):
            c = HALO + t0
            P.add("dve", lambda e: e.tensor_tensor(out=xT[:, m, c:c + w], in0=ps[:, bank, 0:w], in1=xT[:, m, c:c + w],
                                                   op=ALU.add),
                  r=[("ps", bank)], w=[("x", m, t0 // 1024)])

        def emit_ffn(L):
            A.reset()
            hT = A.bf16(KD * 2050).rearrange("p (k t) -> p k t", k=KD)
            aT = [A.bf16(4 * T).rearrange("p (j t) -> p j t", j=4) for _ in range(2)]
            cg = [A.f32(820) for _ in range(3)]
            cv = [A.f32(820) for _ in range(3)]
            sq = A.bf16(KD * 410).rearrange("p (k t) -> p k t", k=KD)
            rt = A.f32(410)
            hk = ("ar", "hT")
            emit_norm(f"nf{L}", 1, 2050, hT, hk, sq, ("ar", "sq"), rt, ("ar", "rt"), 7, 410)

            wup = W("f_w_up")[L].rearrange("(k p) f -> p k f", p=128)
            wdn = W("f_w_down")[L].rearrange("(j p) d -> p j d", p=128)
            v3 = lambda slot: slot.rearrange("p (k f) -> p k f", k=KD)
            d3 = lambda slot: slot.rearrange("p (j d) -> p j d", j=4)
            pu = [0]
            slotc = [0]

            def up_group(g):
                gs, gk = load_piece([(v3, wup[:, :, g * 512:(g + 1) * 512])])
                vs, vk = load_piece([(v3, wup[:, :, 3072 + g * 512:3072 + (g + 1) * 512])])
                gw, vw = v3(gs), v3(vs)
                for jj in range(4):
                    fb = g * 4 + jj
                    for (t_first, nt, w) in UNITS:
                        u = pu[0] % 3
                        pu[0] += 1
                        t0 = TILES[t_first][0]
                        n = nt * w
                        banks = {}
                        for which, wsl, wk in (("g", gw, gk), ("v", vw, vk)):
                            s = slotc[0] % 3
                            slotc[0] += 1
                            banks[which] = s
                            for ti in range(nt):
                                tt0 = TILES[t_first + ti][0]
                                b = 2 * s + ti
                                for k in range(KD):
                                    P.add("pe", lambda e, b=b, wsl=wsl, k=k, jj=jj, tt0=tt0, w=w: e.matmul(
                                        ps[:, b, 0:w + 2], lhsT=wsl[:, k, jj * 128:(jj + 1) * 128],
                                        rhs=hT[:, k, tt0:tt0 + w + 2], start=(k == 0), stop=(k == KD - 1)),
                                        r=[wk, hk], w=[("ps", b)])
                        for which, buf, bkey, off in (("g", cg[u], ("ar", "cg", u), 0), ("v", cv[u], ("ar", "cv", u), 24)):
                            s = banks[which]
                            pk = [("ps", 2 * s + ti) for ti in range(nt)]
                            bv = buf[:, 0:n].rearrange("p (a w) -> p a w", a=nt)
                            mcolidx = fb + off
                            P.add("act", lambda e, s=s, nt=nt, w=w, bv=bv, mi=mcolidx: e.activation(
                                out=bv, in_=ps[:, 2 * s:2 * s + nt, 2:2 + w], func=AF.Identity,
                                scale=ccol(f"fcw{L}_2", mi), bias=ccol(f"fcb{L}", mi)),
                                r=pk + ["consts"], w=[bkey])
                            for tap in (1, 0):
                                P.add("dve", lambda e, s=s, nt=nt, w=w, bv=bv, mi=mcolidx, tap=tap: e.scalar_tensor_tensor(
                                    out=bv, in0=ps[:, 2 * s:2 * s + nt, tap:tap + w], scalar=ccol(f"fcw{L}_{tap}", mi),
                                    in1=bv, op0=ALU.mult, op1=ALU.add),
                                    r=pk + ["consts"], w=[bkey])
                        P.add("act", lambda e, u=u, n=n: e.activation(out=cg[u][:, 0:n], in_=cg[u][:, 0:n],
                                                                     func=AF.Gelu_apprx_tanh),
                              w=[("ar", "cg", u)])
                        P.add("dve", lambda e, u=u, n=n, g=g, jj=jj, t0=t0: e.tensor_tensor(
                            out=aT[g % 2][:, jj, t0:t0 + n], in0=cg[u][:, 0:n], in1=cv[u][:, 0:n], op=ALU.mult),
                            r=[("ar", "cg", u), ("ar", "cv", u)], w=[("ar", "aT", g % 2)])

            dbank = [0]

            def down_group(g):
                ds_, dk = load_piece([(d3, wdn[:, g * 4:(g + 1) * 4, :])])
                dw = d3(ds_)
                for m in range(KD):
                    for tt in range(4):
                        b = 6 + dbank[0] % 2
                        dbank[0] += 1
                        for jj in range(4):
                            P.add("pe", lambda e, b=b, jj=jj, m=m, tt=tt: e.matmul(
                                ps[:, b, :], lhsT=dw[:, jj, m * 128:(m + 1) * 128],
                                rhs=aT[g % 2][:, jj, tt * 512:(tt + 1) * 512], start=(jj == 0), stop=(jj == 3)),
                                r=[dk, ("ar", "aT", g % 2)], w=[("ps", b)])
                        evac_add(b, m, tt * 512, 512)

            NG = int(DBG[DBG.index("FG") + 2]) if "FG" in DBG else 6
            up_group(0)
            for g in range(1, NG):
                up_group(g)
                down_group(g - 1)
            down_group(NG - 1)
            P.fence("ar")
            pool_reset()


        xstate = {"i": 0}
        RG = [list(range(NCORES))]
        x_snd = mcol(24)
        x_rcv = mcol(192)
        x_acc = mcol(24)

        def allgather(src_sb, dst_sb, skey, dkey):
            i = xstate["i"]
            xstate["i"] += 1
            bi, bo = bounce_in[i], bounce_out[i]
            P.add("sp", lambda e: e.dma_start(out=bi.ap()[:, :], in_=src_sb), r=[skey], w=[("dram", "bi", i)], kind="dma")
            P.add("pool", lambda e: e.collective_compute("AllGather", ALU.bypass, replica_groups=RG,
                                                         ins=[bi.ap().opt()], outs=[bo.ap().opt()]),
                  r=[("dram", "bi", i)], w=[("dram", "bo", i)], kind="cc")
            P.add("sp", lambda e: e.dma_start(out=dst_sb.rearrange("p (r f) -> p r f", r=NCORES),
                                              in_=bo.ap().rearrange("(r p) f -> p r f", p=128)),
                  r=[("dram", "bo", i)], w=[dkey], kind="dma")

        def emit_exchange():
            P.add("dve", lambda e: e.tensor_copy(out=x_snd.rearrange("p (k c) -> p k c", k=KD), in_=xT[:, :, XC - 3:XC]),
                  r=[("x", m, 1) for m in range(KD)], w=["xsnd"])
            allgather(x_snd, x_rcv, "xsnd", "xrcv")
            P.add("dve", lambda e: e.tensor_scalar(out=x_acc, in0=x_rcv[:, 0:24], scalar1=ccol("selp", 0), scalar2=None,
                                                   op0=ALU.mult),
                  r=["xrcv", "consts"], w=["xacc"])
            for r_ in range(1, NCORES):
                P.add("dve", lambda e, r_=r_: e.scalar_tensor_tensor(out=x_acc, in0=x_rcv[:, r_ * 24:(r_ + 1) * 24],
                                                                    scalar=ccol("selp", r_), in1=x_acc,
                                                                    op0=ALU.mult, op1=ALU.add),
                      r=["xrcv", "consts"], w=["xacc"])
            P.add("dve", lambda e: e.tensor_copy(out=xT[:, :, 0:HALO], in_=x_acc.rearrange("p (k c) -> p k c", k=KD)),
                  r=["xacc"], w=["xh"])

        def emit_mixA(L):
            j = L // 2
            A.reset()
            hT = A.bf16(KD * 1024).rearrange("p (k t) -> p k t", k=KD)
            gv = A.bf16(8 * 2048).rearrange("p (c f) -> p c f", c=8)
            wsn = A.bf16(8 * 1024).rearrange("p (c f) -> p c f", c=8)
            wsTm = A.f32(1024)
            bsb = A.f32(1024)
            ug = [A.f32(512) for _ in range(2)]
            yTf = [A.bf16(4 * 1024) for _ in range(2)]
            yT = [v.rearrange("p (q t) -> p q t", q=4) for v in yTf]
            sq = yTf[0].rearrange("p (k t) -> p k t", k=KD)
            rt = A.f32(512)
            junk = A.bf16(512)
            ssq = A.f32(32)
            ssum = A.f32(8)
            rstd = A.f32(8)
            hk = ("ar", "hT")
            P.add("sp", lambda e: e.dma_start(out=wsTm, in_=W("wsT")[j]), w=[("ar", "wsTm")], kind="dma")
            P.add("sp", lambda e: e.dma_start(out=bsb, in_=W("bsb")[j]), w=[("ar", "bsb")], kind="dma")
            wv = wsTm.rearrange("p (g t) -> p g t", g=8)
            P.add("pool", lambda e: e.affine_select(out=wv, in_=wv, pattern=[[0, 8], [1, 128]], compare_op=ALU.is_ge,
                                                    fill=0.0, base=0, channel_multiplier=-1),
                  w=[("ar", "wsTm")])
            win = W("a_w_in")[j].rearrange("(k p) f -> p k f", p=128)
            wout = W("a_w_out")[j].rearrange("(q p) d -> p q d", p=128)
            v3 = lambda slot: slot.rearrange("p (k f) -> p k f", k=KD)
            d3 = lambda slot: slot.rearrange("p (q d) -> p q d", q=4)
            cnt = {"v": 0, "u": 0, "o": 0}
            for st in range(2):
                tok0 = st * 1024
                emit_norm(f"nm{L}", HALO + tok0, 1024, hT, hk, sq, ("ar", "yT", 0), rt, ("ar", "rt"), 6, 512)
                for ft in range(4):
                    psl, pk = load_piece([(v3, win[:, :, 2048 + ft * 512:2048 + (ft + 1) * 512])])
                    pw = v3(psl)
                    for c in range(8):
                        b = cnt["v"] % 4
                        cnt["v"] += 1
                        for k in range(KD):
                            P.add("pe", lambda e, b=b, k=k, c=c, pw=pw: e.matmul(
                                ps[:, b, :], lhsT=hT[:, k, c * 128:(c + 1) * 128], rhs=pw[:, k, :],
                                start=(k == 0), stop=(k == KD - 1)),
                                r=[hk, pk], w=[("ps", b)])
                        P.add("act", lambda e, b=b, c=c, ft=ft: e.activation(out=gv[:, c, ft * 512:(ft + 1) * 512],
                                                                            in_=ps[:, b, :], func=AF.Gelu_apprx_tanh),
                              r=[("ps", b)], w=[("ar", "gv", c)])
                        P.add("act", lambda e, c=c, ft=ft: e.activation(out=junk, in_=gv[:, c, ft * 512:(ft + 1) * 512],
                                                                       func=AF.Square,
                                                                       accum_out=ssq[:, c * 4 + ft:c * 4 + ft + 1]),
                              r=[("ar", "gv", c)], w=[("ar", "junk"), ("ar", "ssq")])
                P.add("dve", lambda e: e.tensor_reduce(out=ssum, in_=ssq.rearrange("p (c f) -> p c f", c=8),
                                                       axis=mybir.AxisListType.X, op=ALU.add),
                      r=[("ar", "ssq")], w=[("ar", "ssum")])
                P.add("act", lambda e: e.activation(out=rstd, in_=ssum, func=AF.Sqrt, scale=1.0 / 2048, bias=EPS),
                      r=[("ar", "ssum")], w=[("ar", "rstd")])
                P.add("dve", lambda e: e.reciprocal(out=rstd, in_=rstd), w=[("ar", "rstd")])
                for c in range(8):
                    P.add("dve", lambda e, c=c: e.tensor_scalar(out=wsn[:, c, :], in0=wsTm, scalar1=rstd[:, c:c + 1],
                                                                scalar2=None, op0=ALU.mult),
                          r=[("ar", "wsTm"), ("ar", "rstd")], w=[("ar", "wsn", c)])

                def u_group(fg):
                    psl, pk = load_piece([(v3, win[:, :, fg * 512:(fg + 1) * 512])])
                    pw = v3(psl)
                    for jj in range(4):
                        fi = fg * 4 + jj
                        g = fi // 2
                        for tt in range(2):
                            bu = cnt["u"] % 2
                            bs = 2 + cnt["u"] % 2
                            cnt["u"] += 1
                            for k in range(KD):
                                P.add("pe", lambda e, bu=bu, k=k, jj=jj, tt=tt, pw=pw: e.matmul(
                                    ps[:, bu, :], lhsT=pw[:, k, jj * 128:(jj + 1) * 128],
                                    rhs=hT[:, k, tt * 512:(tt + 1) * 512], start=(k == 0), stop=(k == KD - 1)),
                                    r=[hk, pk], w=[("ps", bu)])
                            for cc in range(4):
                                c = tt * 4 + cc
                                P.add("pe", lambda e, bs=bs, c=c, cc=cc, fi=fi, g=g: e.matmul(
                                    ps[:, bs, cc * 128:(cc + 1) * 128], lhsT=gv[:, c, fi * 128:(fi + 1) * 128],
                                    rhs=wsn[:, c, g * 128:(g + 1) * 128], start=True, stop=True),
                                    r=[("ar", "gv", c), ("ar", "wsn", c)], w=[("ps", bs)])
                            P.add("act", lambda e, bu=bu: e.activation(out=ug[bu], in_=ps[:, bu, :], func=AF.Gelu_apprx_tanh),
                                  r=[("ps", bu)], w=[("ar", "ug", bu)])
                            psv = ps[:, bs, :].rearrange("p (c t) -> p c t", c=4)
                            bb = bsb[:, g * 128:(g + 1) * 128].unsqueeze(1).to_broadcast([128, 4, 128])
                            P.add("dve", lambda e, psv=psv, bb=bb, fi=fi: e.scalar_tensor_tensor(
                                out=psv, in0=psv, scalar=ccol(f"avg{j}", fi), in1=bb, op0=ALU.mult, op1=ALU.add),
                                r=[("ar", "bsb"), "consts"], w=[("ps", bs)])
                            P.add("dve", lambda e, bs=bs, bu=bu, fg=fg, jj=jj, tt=tt: e.tensor_tensor(
                                out=yT[fg % 2][:, jj, tt * 512:(tt + 1) * 512], in0=ps[:, bs, :], in1=ug[bu], op=ALU.mult),
                                r=[("ps", bs), ("ar", "ug", bu)], w=[("ar", "yT", fg % 2)])

                def o_group(fg):
                    osl, ok = load_piece([(d3, wout[:, fg * 4:(fg + 1) * 4, :])])
                    ow = d3(osl)
                    for m in range(KD):
                        for tt in range(2):
                            b = 4 + cnt["o"] % 2
                            cnt["o"] += 1
                            for kk in range(4):
                                P.add("pe", lambda e, b=b, kk=kk, m=m, tt=tt, ow=ow, fg=fg: e.matmul(
                                    ps[:, b, :], lhsT=ow[:, kk, m * 128:(m + 1) * 128],
                                    rhs=yT[fg % 2][:, kk, tt * 512:(tt + 1) * 512], start=(kk == 0), stop=(kk == 3)),
                                    r=[ok, ("ar", "yT", fg % 2)], w=[("ps", b)])
                            evac_add(b, m, tok0 + tt * 512, 512)

                u_group(0)
                for fg in range(1, 4):
                    u_group(fg)
                    o_group(fg - 1)
                o_group(3)
            P.fence("ar")
            pool_reset()

        def emit_mixB(L):
            j = L // 2
            A.reset()
            hT = A.bf16(KD * 2052).rearrange("p (k t) -> p k t", k=KD)
            mk3 = lambda v: v.rearrange("p (c t) -> p c t", c=2)
            Rb = [mk3(A.f32(820)) for _ in range(2)]
            Ib = [mk3(A.f32(820)) for _ in range(2)]
            Xb = [mk3(A.f32(820)) for _ in range(2)]
            xcb = [mk3(A.bf16(820)) for _ in range(2)]
            gsc = mk3(A.f32(820))
            yTf = [A.bf16(2 * T) for _ in range(2)]
            yT = [v.rearrange("p (c t) -> p c t", c=2) for v in yTf]
            sq = yTf[0][:, 0:KD * 412].rearrange("p (k t) -> p k t", k=KD)
            wg = A.bf16(2 * 6 * 2 * 256).rearrange("p (a h k e) -> p a h k e", a=2, h=6, k=2)
            rt = A.f32(412)
            sumr = A.f32(60)
            tmp12 = A.f32(12)
            c8 = A.f32(12)
            c16 = A.f32(12)
            hin = A.f32(12)
            E2 = A.f32(24)
            crcv = A.f32(192)
            t1 = A.f32(12)
            hk = ("ar", "hT")
            for gi, nm in enumerate(("b_w_a", "b_w_x")):
                src = W(nm)[j].rearrange("h (k p) e -> p h k e", p=128)
                P.add("pool", lambda e, gi=gi, src=src: e.dma_start(out=wg[:, gi], in_=src), w=[("ar", "wg")], kind="dma")
            lam = consts[:, cols[f"blam{j}"]:cols[f"blam{j}"] + 12]
            P.add("act", lambda e: e.activation(out=tmp12, in_=lam, func=AF.Exp, scale=-1.0), r=["consts"], w=[("ar", "tmp12")])
            P.add("act", lambda e: e.activation(out=tmp12, in_=tmp12, func=AF.Ln, bias=1.0), w=[("ar", "tmp12")])
            P.add("dve", lambda e: e.tensor_scalar(out=c8, in0=tmp12, scalar1=-8.0, scalar2=None, op0=ALU.mult),
                  r=[("ar", "tmp12")], w=[("ar", "c8")])
            P.add("dve", lambda e: e.tensor_scalar(out=c16, in0=tmp12, scalar1=-16.0, scalar2=None, op0=ALU.mult),
                  r=[("ar", "tmp12")], w=[("ar", "c8")])
            emit_norm(f"nm{L}", 0, XC, hT, hk, sq, ("ar", "yT", 0), rt, ("ar", "rt"), 7, 412)

            win = W("b_w_in")[j].rearrange("(k p) f -> p k f", p=128)
            wout = W("b_w_out")[j].rearrange("(q p) d -> p q d", p=128)
            ucnt = [0]
            ocnt = [0]
            cw = lambda tap, ci: ccol(f"bcw{j}_{tap}", ci)

            def run_pass(pno):
                for h in range(6):
                    xv = lambda slot: slot[:, 0:2048].rearrange("p (k f) -> p k f", k=KD)
                    gvw = lambda slot: slot[:, 2048:4096].rearrange("p (k f) -> p k f", k=KD)
                    dm = [(xv, win[:, :, 1536 + h * 256:1536 + (h + 1) * 256])]
                    if pno == 2:
                        dm.append((gvw, win[:, :, h * 256:(h + 1) * 256]))
                    psl, pk = load_piece(dm)
                    xw, gw = xv(psl), gvw(psl)
                    for ti, (t0, w) in enumerate(TILES):
                        par = ucnt[0] % 2
                        ucnt[0] += 1
                        R_, I_, X_ = Rb[par], Ib[par], Xb[par]
                        kR, kI, kX, kC = ("ar", "R", par), ("ar", "I", par), ("ar", "X", par), ("ar", "xcb", par)
                        for ct in range(2):
                            ci = 2 * h + ct
                            for k in range(KD):
                                P.add("pe", lambda e, ct=ct, k=k, t0=t0, w=w, xw=xw: e.matmul(
                                    ps[:, ct, 0:w + 3], lhsT=xw[:, k, ct * 128:(ct + 1) * 128],
                                    rhs=hT[:, k, t0:t0 + w + 3], start=(k == 0), stop=(k == KD - 1)),
                                    r=[hk, pk], w=[("ps", ct)])
                            P.add("act", lambda e, ct=ct, ci=ci, w=w, X_=X_: e.activation(
                                out=X_[:, ct, 0:w], in_=ps[:, ct, 3:3 + w], func=AF.Identity,
                                scale=cw(3, ci), bias=ccol(f"bcb{j}", ci)),
                                r=[("ps", ct), "consts"], w=[kX])
                            for tap in (2, 1, 0):
                                P.add("dve", lambda e, ct=ct, ci=ci, w=w, X_=X_, tap=tap: e.scalar_tensor_tensor(
                                    out=X_[:, ct, 0:w], in0=ps[:, ct, tap:tap + w], scalar=cw(tap, ci),
                                    in1=X_[:, ct, 0:w], op0=ALU.mult, op1=ALU.add),
                                    r=[("ps", ct), "consts"], w=[kX])
                        P.add("act" if 'P' in DBG else "pool", lambda e, par=par, w=w, X_=X_: (e.copy if 'P' in DBG else e.tensor_copy)(out=xcb[par][:, :, 0:w], in_=X_[:, :, 0:w]),
                              r=[kX], w=[kC])
                        if 'U1' in DBG:
                            continue
                        for et in range(2):
                            ei = 2 * h + et
                            for gi, (G_, kG, bn) in enumerate(((R_, kR, f"bba{j}"), (I_, kI, f"bbx{j}"))):
                                b = 2 + gi * 2 + et
                                for k in range(2):
                                    P.add("pe", lambda e, b=b, gi=gi, k=k, et=et, w=w, par=par, h=h: e.matmul(
                                        ps[:, b, 0:w], lhsT=wg[:, gi, h, k, et * 128:(et + 1) * 128],
                                        rhs=xcb[par][:, k, 0:w], start=(k == 0), stop=(k == 1)),
                                        r=[("ar", "wg"), kC], w=[("ps", b)])
                                if gi == 0 and pno == 1 and 'Q' not in DBG:
                                    P.add("act", lambda e, b=b, et=et, ei=ei, w=w, G_=G_, bn=bn, ti=ti: e.activation(
                                        out=G_[:, et, 0:w], in_=ps[:, b, 0:w], func=AF.Sigmoid, bias=ccol(bn, ei),
                                        accum_out=sumr[:, ei * 5 + ti:ei * 5 + ti + 1]),
                                        r=[("ps", b), "consts"], w=[kG, ("ar", "sumr")])
                                else:
                                    P.add("act", lambda e, b=b, et=et, ei=ei, w=w, G_=G_, bn=bn: e.activation(
                                        out=G_[:, et, 0:w], in_=ps[:, b, 0:w], func=AF.Sigmoid, bias=ccol(bn, ei)),
                                        r=[("ps", b), "consts"], w=[kG])
                        if 'U2' in DBG:
                            continue
                        P.add("dve", lambda e, w=w, I_=I_, X_=X_: e.tensor_tensor(out=I_[:, :, 0:w], in0=I_[:, :, 0:w],
                                                                                 in1=X_[:, :, 0:w], op=ALU.mult),
                              r=[kX], w=[kI])
                        for et in range(2):
                            ei = 2 * h + et
                            P.add("act", lambda e, et=et, ei=ei, w=w, X_=X_, R_=R_: e.activation(
                                out=X_[:, et, 0:w], in_=R_[:, et, 0:w], func=AF.Exp, scale=c16[:, ei:ei + 1]),
                                r=[kR, ("ar", "c8")], w=[kX])
                        P.add("act", lambda e, w=w, X_=X_: e.activation(out=X_[:, :, 0:w], in_=X_[:, :, 0:w], func=AF.Sqrt,
                                                                        scale=-1.0, bias=1.0),
                              w=[kX])
                        P.add("dve", lambda e, w=w, I_=I_, X_=X_: e.tensor_tensor(out=I_[:, :, 0:w], in0=I_[:, :, 0:w],
                                                                                 in1=X_[:, :, 0:w], op=ALU.mult),
                              r=[kX], w=[kI])
                        for et in range(2):
                            ei = 2 * h + et
                            P.add("act", lambda e, et=et, ei=ei, w=w, R_=R_: e.activation(
                                out=R_[:, et, 0:w], in_=R_[:, et, 0:w], func=AF.Exp, scale=c8[:, ei:ei + 1]),
                                r=[("ar", "c8")], w=[kR])
                        for et in range(2):
                            ei = 2 * h + et
                            if ti == 0:
                                init = 0.0 if pno == 1 else hin[:, ei:ei + 1]
                                rk = [] if pno == 1 else [("ar", "hin")]
                            else:
                                init = Xb[1 - par][:, et, 409:410]
                                rk = [("ar", "X", 1 - par)]
                            if 'S' in DBG:
                                continue
                            P.add("dve", lambda e, et=et, w=w, X_=X_, R_=R_, I_=I_, init=init: e.tensor_tensor_scan(
                                out=X_[:, et, 0:w], data0=R_[:, et, 0:w], data1=I_[:, et, 0:w], initial=init,
                                op0=ALU.mult, op1=ALU.add),
                                r=[kR, kI] + rk, w=[kX])
                            if pno == 1 and ti == len(TILES) - 1:
                                P.add("dve", lambda e, et=et, ei=ei, w=w, X_=X_: e.tensor_copy(
                                    out=E2[:, ei:ei + 1], in_=X_[:, et, w - 1:w]),
                                    r=[kX], w=[("ar", "E2")])
                        if pno == 2:
                            for ct in range(2):
                                for k in range(KD):
                                    P.add("pe", lambda e, ct=ct, k=k, t0=t0, w=w, gw=gw: e.matmul(
                                        ps[:, 6 + ct, 0:w + 1], lhsT=gw[:, k, ct * 128:(ct + 1) * 128],
                                        rhs=hT[:, k, t0 + 2:t0 + 3 + w], start=(k == 0), stop=(k == KD - 1)),
                                        r=[hk, pk], w=[("ps", 6 + ct)])
                            P.add("act", lambda e, w=w: e.activation(out=gsc[:, :, 0:w], in_=ps[:, 6:8, 1:1 + w],
                                                                     func=AF.Gelu_apprx_tanh),
                                  r=[("ps", 6), ("ps", 7)], w=[("ar", "gsc")])
                            P.add("dve", lambda e, w=w, t0=t0, h=h, X_=X_: e.tensor_tensor(
                                out=yT[h % 2][:, :, t0:t0 + w], in0=gsc[:, :, 0:w], in1=X_[:, :, 0:w], op=ALU.mult),
                                r=[("ar", "gsc"), kX], w=[("ar", "yT", h % 2)])
                    if pno == 2:
                        osl, ok = load_piece([(lambda slot: slot[:, 0:2048].rearrange("p (q d) -> p q d", q=2),
                                               wout[:, 2 * h:2 * h + 2, :])])
                        ow = osl[:, 0:2048].rearrange("p (q d) -> p q d", q=2)
                        for m in range(KD):
                            for tt in range(4):
                                b = 6 + ocnt[0] % 2
                                ocnt[0] += 1
                                for ct in range(2):
                                    P.add("pe", lambda e, b=b, ct=ct, m=m, tt=tt, ow=ow, h=h: e.matmul(
                                        ps[:, b, :], lhsT=ow[:, ct, m * 128:(m + 1) * 128],
                                        rhs=yT[h % 2][:, ct, tt * 512:(tt + 1) * 512], start=(ct == 0), stop=(ct == 1)),
                                        r=[ok, ("ar", "yT", h % 2)], w=[("ps", b)])
                                evac_add(b, m, tt * 512, 512)

            if 'L0' in DBG:
                P.fence('ar')
                return
            run_pass(1)
            if 'L1' in DBG:
                P.fence('ar')
                return
            P.add("dve", lambda e: e.tensor_reduce(out=tmp12, in_=sumr.rearrange("p (c f) -> p c f", c=12),
                                                   axis=mybir.AxisListType.X, op=ALU.add),
                  r=[("ar", "sumr")], w=[("ar", "tmp12")])
            P.add("dve", lambda e: e.tensor_tensor(out=tmp12, in0=tmp12, in1=c8, op=ALU.mult),
                  r=[("ar", "c8")], w=[("ar", "tmp12")])
            P.add("act", lambda e: e.activation(out=E2[:, 12:24], in_=tmp12, func=AF.Exp),
                  r=[("ar", "tmp12")], w=[("ar", "E2")])
            allgather(E2, crcv, ("ar", "E2"), ("ar", "crcv"))
            P.add("dve", lambda e: e.memset(hin, 0.0), w=[("ar", "hin")])
            for r_ in range(NCORES):
                hl = crcv[:, r_ * 24:r_ * 24 + 12]
                ae = crcv[:, r_ * 24 + 12:r_ * 24 + 24]
                msk = ccol("selq", r_)
                P.add("dve", lambda e, ae=ae, msk=msk: e.tensor_scalar(out=t1, in0=ae, scalar1=-1.0, scalar2=msk,
                                                                        op0=ALU.add, op1=ALU.mult),
                      r=[("ar", "crcv"), "consts"], w=[("ar", "t1")])
                P.add("dve", lambda e: e.scalar_tensor_tensor(out=hin, in0=t1, scalar=1.0, in1=hin, op0=ALU.add, op1=ALU.mult),
                      r=[("ar", "t1")], w=[("ar", "hin")])
                P.add("dve", lambda e, hl=hl, msk=msk: e.scalar_tensor_tensor(out=hin, in0=hl, scalar=msk, in1=hin,
                                                                             op0=ALU.mult, op1=ALU.add),
                      r=[("ar", "crcv"), "consts"], w=[("ar", "hin")])
            if 'L2' in DBG:
                P.fence('ar')
                return
            run_pass(2)
            P.fence("ar")
            pool_reset()
        def emit_final(normed):
            A.reset()
            sq = A.bf16(KD * 512).rearrange("p (k t) -> p k t", k=KD)
            rt = A.f32(512)
            ob = [A.f32(KD * 512).rearrange("p (k t) -> p k t", k=KD) for _ in range(2)]
            yv = y_out.rearrange("(k p) t -> p k t", p=128)
            last = []
            for tt in range(4):
                o = ob[tt % 2]
                ok = ("ar", "ob", tt % 2)
                if normed:
                    emit_norm("nfin", HALO + tt * 512, 512, o, ok, sq, ("ar", "sq"), rt, ("ar", "rt"), 7, 512)
                    op = P.add("sp", lambda e, o=o, tt=tt: e.dma_start(out=yv[:, :, tt * 512:(tt + 1) * 512], in_=o[:, :, :]),
                               r=[ok], kind="dma")
                else:
                    c = HALO + tt * 512
                    op = P.add("sp", lambda e, c=c, tt=tt: e.dma_start(out=yv[:, :, tt * 512:(tt + 1) * 512], in_=xT[:, :, c:c + 512]),
                               r=xkeys(c, c + 512), kind="dma")
                last.append(op)
            P.add("sp", None, extra=last)
            P.fence("ar")

        for ph in phases:
            if ph[0] == "F":
                emit_ffn(ph[1])
            elif ph[0] == "A":
                emit_mixA(ph[1])
            elif ph[0] == "B":
                emit_mixB(ph[1])
            elif ph[0] == "X":
                emit_exchange()
            elif ph[0] == "N":
                emit_final(True)
            elif ph[0] == "S":
                emit_final(False)
            else:
                raise ValueError(ph)

        P.finalize(eng_sems, dma_sems, cc_sem)
        with nc.Block() as block:
            @block.tensor
            def _(e):
                P.emit("pe", e, eng_sems)

            @block.scalar
            def _(e):
                P.emit("act", e, eng_sems)

            @block.vector
            def _(e):
                P.emit("dve", e, eng_sems)

            @block.gpsimd
            def _(e):
                P.emit("pool", e, eng_sems)

            @block.sync
            def _(e):
                P.emit("sp", e, eng_sems)
    return nc, list(used_inputs.keys())


def prep_inputs(inp):
    cols, cons = pack_consts(inp)
    x = np.asarray(inp["x"], np.float32)
    shared = {
        "bsb": np.ascontiguousarray(np.broadcast_to(
            np.asarray(inp["a_b_s"], np.float32).reshape(2, 1, 1024), (2, 128, 1024))),
        "wsT": np.ascontiguousarray(np.transpose(np.asarray(inp["a_w_s"], np.float32), (0, 3, 1, 2)).reshape(2, 128, 1024)),
    }
    for k in ("a_w_in", "a_w_out", "b_w_in", "b_w_out", "b_w_a", "b_w_x", "f_w_up", "f_w_down"):
        shared[k] = np.ascontiguousarray(np.asarray(inp[k], np.float32))
    xts = []
    for c in range(NCORES):
        b, p = c // 4, c % 4
        xts.append(np.ascontiguousarray(x[b, p * T:(p + 1) * T, :].T))
    return cols, cons, shared, xts


def run_phases(phases, cols, cons, shared, xts, trace=False):
    nc, used = build_program(phases, cols, cons[0].shape[1])
    in_maps = []
    for c in range(NCORES):
        m = {k: shared[k] for k in used if k in shared}
        m["consts"] = cons[c]
        m["xT"] = xts[c]
        in_maps.append(m)
    res = run_bass_kernel_spmd(nc, in_maps, core_ids=list(range(NCORES)), trace=trace)
    return [np.asarray(r["yT"]) for r in res.results], res


FULL = [("A", 0), ("X", 2), ("F", 0), ("X", 3), ("B", 1), ("X", 2), ("F", 1),
        ("A", 2), ("X", 2), ("F", 2), ("X", 3), ("B", 3), ("X", 2), ("F", 3), ("N",)]


def kernel(**inputs):
    cols, cons, shared, xts = prep_inputs(inputs)
    outs, _ = run_phases(FULL, cols, cons, shared, xts)
    y = np.empty((2, 4 * T, D), np.float32)
    for c in range(NCORES):
        y[c // 4, (c % 4) * T:(c % 4 + 1) * T, :] = outs[c].T
    return y
```
